# Optimizing a Trainium2 kernel written in Bass

```python
import jax, jax.numpy as jnp
from jax import lax
import numpy as np

D_MODEL = 1024
BATCH = 2
SEQ = 8192
DEPTH = 1

CHUNK = 64
Q_BLOCK = 128

D_A = D_MODEL // 2
N_A = 64
H_A = D_A // N_A
LORA_W = max(32, int(round(1.8 * D_A ** 0.5 / 32)) * 32)
LORA_A = max(32, int(round(1.8 * D_A ** 0.5 / 32)) * 32)
LORA_G = max(32, int(round(0.6 * D_A ** 0.8 / 32)) * 32)
LNX_EPS = 64e-5

D_B = D_MODEL // 2
HD_B = 64
H_B = D_B // HD_B

D_FF = ((8 * D_MODEL // 3 + 255) // 256) * 256
RMS_EPS = 1e-6

SHIFT_COLS = 3 * D_A + LORA_W + LORA_A + LORA_G
FOX_COLS = 3 * D_B + H_B
IN_COLS = SHIFT_COLS + FOX_COLS + 2 * D_MODEL

kernel_name = "rwkv7_fox_gated_macaron_block"


def rms_norm(x, g):
    xf = x.astype(jnp.float32)
    y = xf * lax.rsqrt(jnp.mean(xf * xf, axis=-1, keepdims=True) + RMS_EPS)
    return (y * g.astype(jnp.float32)).astype(x.dtype)


def swiglu(h, w1, w3, w2):
    return (jax.nn.silu(h @ w1) * (h @ w3)) @ w2


def token_shift(u, mu):
    prev = jnp.pad(u, ((0, 0), (1, 0), (0, 0)))[:, :-1]
    return u + (prev - u) * mu


def rwkv7_recurrence(r, w, k, v, a, b):
    B, S, H, N = r.shape
    n_chunks = S // CHUNK

    def to_chunks(t):
        return jnp.moveaxis(t, 1, 0).reshape(n_chunks, CHUNK, B, H, N)

    def step(state, inp):
        r_t, w_t, k_t, v_t, a_t, b_t = inp
        sa = jnp.einsum('bhvk,bhk->bhv', state, a_t)
        state = (state * w_t[:, :, None, :]
                 + sa[..., :, None] * b_t[:, :, None, :]
                 + v_t[..., :, None] * k_t[:, :, None, :])
        y_t = jnp.einsum('bhvk,bhk->bhv', state, r_t)
        return state, y_t

    def chunk_step(state, chunk_inp):
        return lax.scan(step, state, chunk_inp)

    xs = tuple(to_chunks(t) for t in (r, w, k, v, a, b))
    state0 = jnp.zeros((B, H, N, N), jnp.float32)
    _, ys = lax.scan(chunk_step, state0, xs)
    return jnp.moveaxis(ys.reshape(S, B, H, N), 0, 1)


def rwkv7_mixer(p, w0, w_up, a0, a_up, g_up, k_k, k_a, r_k, lnx_g, lnx_b):
    B, S, _ = p.shape
    f32 = jnp.float32
    r, k, v, w_lo, a_lo, g_lo = jnp.split(
        p, [D_A, 2 * D_A, 3 * D_A, 3 * D_A + LORA_W, 3 * D_A + LORA_W + LORA_A], axis=-1)
    w_log = -jax.nn.softplus(-(w0 + jnp.tanh(w_lo) @ w_up)) - 0.5
    decay = jnp.exp(-jnp.exp(w_log.astype(f32)))
    a = jax.nn.sigmoid(a0 + a_lo @ a_up)
    g = jax.nn.sigmoid(g_lo) @ g_up

    def heads(t):
        return t.astype(f32).reshape(B, S, H_A, N_A)

    r, k, v, a, decay = heads(r), heads(k), heads(v), heads(a), heads(decay)
    kk = k * k_k.astype(f32).reshape(H_A, N_A)
    kk = kk / jnp.maximum(jnp.sqrt(jnp.sum(kk * kk, axis=-1, keepdims=True)), 1e-12)
    k = k * (1.0 + (a - 1.0) * k_a.astype(f32).reshape(H_A, N_A))
    y = rwkv7_recurrence(r, decay, k, v, -kk, kk * a)
    mu = jnp.mean(y, axis=-1, keepdims=True)
    var = jnp.mean(jnp.square(y - mu), axis=-1, keepdims=True)
    y = ((y - mu) * lax.rsqrt(var + LNX_EPS)).reshape(B, S, D_A)
    y = y * lnx_g.astype(f32) + lnx_b.astype(f32)
    bonus = jnp.sum(r * k * r_k.astype(f32), axis=-1, keepdims=True) * v
    y = (y + bonus.reshape(B, S, D_A)) * g.astype(f32)
    return y.astype(p.dtype)


def forgetting_attention(q, k, v, log_f):
    B, H, S, hd = q.shape
    nb = S // Q_BLOCK
    scale = 1.0 / np.sqrt(hd).astype(np.float32)
    cum = jnp.cumsum(log_f, axis=-1)
    q_blocks = q.reshape(B, H, nb, Q_BLOCK, hd).transpose(2, 0, 1, 3, 4)
    c_blocks = cum.reshape(B, H, nb, Q_BLOCK).transpose(2, 0, 1, 3)
    pos_k = jnp.arange(S)

    def one_block(args):
        i, q_i, c_i = args
        s = jnp.einsum('bhqd,bhkd->bhqk', q_i, k, preferred_element_type=jnp.float32) * scale
        s = s + c_i[..., :, None] - cum[..., None, :]
        pos_q = i * Q_BLOCK + jnp.arange(Q_BLOCK)
        mask = pos_k[None, :] <= pos_q[:, None]
        s = jnp.where(mask, s, -jnp.inf)
        pr = jax.nn.softmax(s, axis=-1)
        return jnp.einsum('bhqk,bhkd->bhqd', pr.astype(v.dtype), v)

    out = lax.map(one_block, (jnp.arange(nb), q_blocks, c_blocks))
    return out.transpose(1, 2, 0, 3, 4).reshape(B, H, S, hd)


def fox_mixer(p, f_bias, q_norm, k_norm):
    B, S, _ = p.shape
    q, k, v, f_lo = jnp.split(p, [D_B, 2 * D_B, 3 * D_B], axis=-1)

    def heads(t):
        return t.reshape(B, S, H_B, HD_B).transpose(0, 2, 1, 3)

    q = rms_norm(heads(q), q_norm)
    k = rms_norm(heads(k), k_norm)
    v = heads(v)
    log_f = jax.nn.log_sigmoid((f_lo + f_bias).astype(jnp.float32)).transpose(0, 2, 1)
    o = forgetting_attention(q, k, v, log_f)
    return o.transpose(0, 2, 1, 3).reshape(B, S, D_B)


def setup_inputs(seed: int = 0) -> dict:
    key = jax.random.key(seed)
    ks = jax.random.split(key, 28)
    L = DEPTH
    f32 = jnp.float32

    def nrm(k, shape, scale):
        return jax.random.normal(k, shape, f32) * scale

    def gain(k, shape):
        return 1.0 + 0.05 * jax.random.normal(k, shape, f32)

    return {
        "x": nrm(ks[0], (BATCH, SEQ, D_MODEL), 1.0),
        "ffn1_norm": gain(ks[1], (L, D_MODEL)),
        "ffn1_w1": nrm(ks[2], (L, D_MODEL, D_FF), D_MODEL ** -0.5),
        "ffn1_w3": nrm(ks[3], (L, D_MODEL, D_FF), D_MODEL ** -0.5),
        "ffn1_w2": nrm(ks[4], (L, D_FF, D_MODEL), D_FF ** -0.5),
        "mix_norm": gain(ks[5], (L, D_MODEL)),
        "w_in": nrm(ks[6], (L, D_MODEL, IN_COLS), D_MODEL ** -0.5),
        "shift_mu": jax.random.uniform(ks[7], (L, SHIFT_COLS), f32),
        "rwkv_w0": jax.random.uniform(ks[8], (L, D_A), f32, -6.5, -1.5),
        "rwkv_w_up": nrm(ks[9], (L, LORA_W, D_A), 0.5 * LORA_W ** -0.5),
        "rwkv_a0": nrm(ks[10], (L, D_A), 0.1),
        "rwkv_a_up": nrm(ks[11], (L, LORA_A, D_A), LORA_A ** -0.5),
        "rwkv_g_up": nrm(ks[12], (L, LORA_G, D_A), LORA_G ** -0.5),
        "rwkv_k_k": 0.85 + 0.05 * jax.random.normal(ks[13], (L, D_A), f32),
        "rwkv_k_a": gain(ks[14], (L, D_A)),
        "rwkv_r_k": nrm(ks[15], (L, H_A, N_A), 0.1),
        "rwkv_lnx_g": gain(ks[16], (L, D_A)),
        "rwkv_lnx_b": nrm(ks[17], (L, D_A), 0.02),
        "rwkv_proj": nrm(ks[18], (L, D_A, D_MODEL), D_A ** -0.5),
        "fox_f_bias": 2.0 + 0.5 * jax.random.normal(ks[19], (L, H_B), f32),
        "fox_q_norm": gain(ks[20], (L, HD_B)),
        "fox_k_norm": gain(ks[21], (L, HD_B)),
        "fox_proj": nrm(ks[22], (L, D_B, D_MODEL), D_B ** -0.5),
        "w_out": nrm(ks[23], (L, D_MODEL, D_MODEL), D_MODEL ** -0.5),
        "ffn2_norm": gain(ks[24], (L, D_MODEL)),
        "ffn2_w1": nrm(ks[25], (L, D_MODEL, D_FF), D_MODEL ** -0.5),
        "ffn2_w3": nrm(ks[26], (L, D_MODEL, D_FF), D_MODEL ** -0.5),
        "ffn2_w2": nrm(ks[27], (L, D_FF, D_MODEL), D_FF ** -0.5),
        "final_norm": gain(jax.random.fold_in(key, 99), (D_MODEL,)),
    }


def reference(x, ffn1_norm, ffn1_w1, ffn1_w3, ffn1_w2, mix_norm, w_in, shift_mu,
              rwkv_w0, rwkv_w_up, rwkv_a0, rwkv_a_up, rwkv_g_up, rwkv_k_k, rwkv_k_a, rwkv_r_k,
              rwkv_lnx_g, rwkv_lnx_b, rwkv_proj, fox_f_bias, fox_q_norm, fox_k_norm, fox_proj,
              w_out, ffn2_norm, ffn2_w1, ffn2_w3, ffn2_w2, final_norm):
    for l in range(DEPTH):
        x = x + 0.5 * swiglu(rms_norm(x, ffn1_norm[l]), ffn1_w1[l], ffn1_w3[l], ffn1_w2[l])

        h = rms_norm(x, mix_norm[l])
        p = h @ w_in[l]
        p_rwkv = token_shift(p[..., :SHIFT_COLS], shift_mu[l])
        p_fox = p[..., SHIFT_COLS:SHIFT_COLS + FOX_COLS]
        gates = jax.nn.sigmoid(p[..., SHIFT_COLS + FOX_COLS:])
        gate_a, gate_b = gates[..., :D_MODEL], gates[..., D_MODEL:]

        y_a = rwkv7_mixer(p_rwkv, rwkv_w0[l], rwkv_w_up[l], rwkv_a0[l], rwkv_a_up[l], rwkv_g_up[l],
                          rwkv_k_k[l], rwkv_k_a[l], rwkv_r_k[l], rwkv_lnx_g[l], rwkv_lnx_b[l]) @ rwkv_proj[l]
        y_b = fox_mixer(p_fox, fox_f_bias[l], fox_q_norm[l], fox_k_norm[l]) @ fox_proj[l]
        x = x + (gate_a * y_a + gate_b * y_b) @ w_out[l]

        x = x + 0.5 * swiglu(rms_norm(x, ffn2_norm[l]), ffn2_w1[l], ffn2_w3[l], ffn2_w2[l])
    return rms_norm(x, final_norm)
```

```python
import numpy as np
from contextlib import ExitStack
import ml_dtypes
import concourse.bass as bass
import concourse.mybir as mybir
from concourse.bass_utils import run_bass_kernel_spmd

F32 = mybir.dt.float32
BF16 = mybir.dt.bfloat16
AF = mybir.ActivationFunctionType
ALU = mybir.AluOpType

D = 1024
S = 8192
B = 2
DFF = 2816
NJ = DFF // 128
KC = D // 128
NCORES = 8
TOK = 2048
UNIT = 1024
RMS_EPS = 1e-6
LNX_EPS = 64e-5
SHIFT_COLS = 1696
FOX_COLS = 1544


class Sched:
    COMPUTE = ("pe", "act", "dve", "pool")

    def __init__(self, nc, es, n_dma_sems=12):
        self.nc = nc
        self.eng = {"pe": nc.tensor, "act": nc.scalar, "dve": nc.vector, "pool": nc.gpsimd, "sp": nc.sync}
        self.ops = {e: [] for e in self.eng}
        self.cnt = {e: 0 for e in self.COMPUTE}
        self.sem = {e: es.enter_context(nc.semaphore("sem_" + e)) for e in self.COMPUTE}
        self.dsem = [es.enter_context(nc.semaphore("dsem%d" % i)) for i in range(n_dma_sems)]
        self.dval = [0] * n_dma_sems
        self.dnext = 0
        self.seen = {e: {} for e in self.eng}
        self.last_w = {}
        self.readers = {}
        self.all_dma = []
        self.excl = {"pp", "pm", "pm2", "pst0", "pst1", "poT", "A", "B", "pa0", "pa1", "pb0", "pb1", "po0", "po1", "pss",
                     "q0", "q1", "q2", "q3", "q4", "q5", "q6", "q7"}

    def _semobj(self, key):
        if isinstance(key, tuple):
            return self.dsem[key[1]]
        return self.sem[key]

    def _wait(self, e, key, val):
        if self.seen[e].get(key, 0) >= val:
            return
        self.seen[e][key] = val
        so = self._semobj(key)
        self.ops[e].append(lambda eng, so=so, val=val: eng.wait_ge(so, val))

    def op(self, e, fn, reads=(), writes=(), dma=False):
        deps = {}
        writes = list(writes) + [k for k in reads if k in self.excl and k not in writes]

        def add(tok):
            if tok is None:
                return
            k, v = tok
            if deps.get(k, 0) < v:
                deps[k] = v

        for k in reads:
            add(self.last_w.get(k))
        for k in writes:
            add(self.last_w.get(k))
            for rk, rv in self.readers.get(k, {}).items():
                add((rk, rv))
        for k, v in deps.items():
            if k == "pe" and e == "pe" and not dma:
                continue
            self._wait(e, k, v)
        if dma:
            i = self.dnext
            self.dnext = (self.dnext + 1) % len(self.dsem)
            if self.dval[i] > 0:
                self._wait(e, ("d", i), self.dval[i])
            self.dval[i] += 16
            tok = (("d", i), self.dval[i])
            so = self.dsem[i]
            self.ops[e].append(lambda eng, fn=fn, so=so: fn(eng).then_inc(so, 16))
            self.all_dma.append(tok)
        else:
            self.cnt[e] += 1
            tok = (e, self.cnt[e])
            so = self.sem[e]
            self.ops[e].append(lambda eng, fn=fn, so=so: fn(eng).then_inc(so, 1))
        for k in reads:
            r = self.readers.setdefault(k, {})
            if r.get(tok[0], 0) < tok[1]:
                r[tok[0]] = tok[1]
        for k in writes:
            self.last_w[k] = tok
            self.readers[k] = {}
        return tok

    def finish(self):
        for i, v in enumerate(self.dval):
            if v > 0:
                self._wait("sp", ("d", i), v)
        for e in self.COMPUTE:
            if self.cnt[e] > 0:
                self._wait("sp", e, self.cnt[e])

    def emit(self):
        self.finish()
        with self.nc.Block() as block:
            for e, deco in (("pe", block.tensor), ("act", block.scalar), ("dve", block.vector),
                            ("pool", block.gpsimd), ("sp", block.sync)):
                ops = self.ops[e]

                def body(eng, ops=ops):
                    for f in ops:
                        f(eng)
                deco(body)


def _mk(nc, es, kind, name, shape, dt):
    f = nc.sbuf_tensor if kind == "sb" else nc.psum_tensor
    return es.enter_context(f(("s_" if kind == "sb" else "p_") + name, list(shape), dt))


class FFNCtx:
    def __init__(self, nc, es, sc):
        self.nc, self.sc = nc, sc
        sb = lambda n, s, d: _mk(nc, es, "sb", n, s, d)
        ps = lambda n, s, d: _mk(nc, es, "ps", n, s, d)
        self.xT = sb("xT", [128, KC, UNIT], F32)
        self.hT = sb("hT", [128, KC, UNIT], BF16)
        self.u = sb("u", [128, NJ, UNIT], BF16)
        self.sq = sb("sq", [128, KC, 512], BF16)
        self.rstd = [sb("rstd%d" % i, [128, 512], F32) for i in range(2)]
        self.w13s = [sb("w13s%d" % i, [128, 2, KC, 128], F32) for i in range(2)]
        self.w13b = [sb("w13b%d" % i, [128, 2, KC, 128], BF16) for i in range(2)]
        self.w2s = [sb("w2s%d" % i, [128, NJ, 128], F32) for i in range(2)]
        self.w2b = [sb("w2b%d" % i, [128, NJ, 128], BF16) for i in range(2)]
        self.sil = [sb("sil%d" % i, [128, 512], F32) for i in range(2)]
        self.ones = sb("ones", [128, 128], BF16)
        self.gv = sb("gv", [128, 4, KC], F32)
        self.pa = [ps("pa%d" % i, [128, 512], F32) for i in range(2)]
        self.pb = [ps("pb%d" % i, [128, 512], F32) for i in range(2)]
        self.po = [ps("po%d" % i, [128, 512], F32) for i in range(2)]
        self.pss = ps("pss", [128, 512], F32)
        self.n_norm = 0
        self.n_w13 = 0
        self.n_w2 = 0
        self.n_sil = 0
        self.n_po = 0
        sc.op("pool", lambda e: e.memset(self.ones[:], 1.0), writes=["ones"])

    def rmsnorm(self, gidx, ntok=UNIT, inplace=False):
        sc = self.sc
        for tt in range(ntok // 512):
            ts = slice(tt * 512, (tt + 1) * 512)
            sc.op("act", lambda e, ts=ts: e.activation(out=self.sq[:], in_=self.xT[:, :, ts], func=AF.Square),
                  reads=["xT"], writes=["sq"])
            for kc in range(KC):
                sc.op("pe", lambda e, kc=kc: e.matmul(self.pss[:], lhsT=self.ones[:], rhs=self.sq[:, kc, :],
                                                      start=(kc == 0), stop=(kc == KC - 1)),
                      reads=["ones", "sq"], writes=["pss"])
            rs = self.rstd[self.n_norm % 2]
            rk = "rstd%d" % (self.n_norm % 2)
            self.n_norm += 1
            sc.op("act", lambda e, rs=rs: e.activation(out=rs[:], in_=self.pss[:], func=AF.Sqrt,
                                                       scale=1.0 / D, bias=self.eps_ap()),
                  reads=["pss", "eps"], writes=[rk])
            sc.op("dve", lambda e, rs=rs: e.reciprocal(out=rs[:], in_=rs[:]), reads=[rk], writes=[rk])
            for kc in range(KC):
                dst = self.xT if inplace else self.hT
                sc.op("dve", lambda e, kc=kc, rs=rs, ts=ts, dst=dst: e.scalar_tensor_tensor(
                    out=dst[:, kc, ts], in0=self.xT[:, kc, ts], scalar=self.gv[:, gidx, kc:kc + 1],
                    in1=rs[:], op0=ALU.mult, op1=ALU.mult),
                    reads=["xT", "gv", rk], writes=["xT" if inplace else "hT"])

    def eps_ap(self):
        return self.epsT[:, 0:1]

    def ffn(self, w13_dram, w2_dram, q="sp"):
        sc = self.sc
        ntile = UNIT // 512
        for j in range(NJ):
            i = self.n_w13 % 2
            self.n_w13 += 1
            ws, wb = self.w13s[i], self.w13b[i]
            sc.op(q, lambda e, ws=ws, j=j: e.dma_start(out=ws[:], in_=w13_dram[j]), writes=["w13s%d" % i], dma=True)
            sc.op("pool", lambda e, ws=ws, wb=wb: e.tensor_copy(out=wb[:], in_=ws[:]),
                  reads=["w13s%d" % i], writes=["w13b%d" % i])
            for tt in range(ntile):
                ts = slice(tt * 512, (tt + 1) * 512)
                p = (j * ntile + tt) % 2
                pa, pb = self.pa[p], self.pb[p]
                for kc in range(KC):
                    sc.op("pe", lambda e, kc=kc, wb=wb, pa=pa, ts=ts: e.matmul(
                        pa[:], lhsT=wb[:, 0, kc, :], rhs=self.hT[:, kc, ts], start=(kc == 0), stop=(kc == KC - 1)),
                        reads=["w13b%d" % i, "hT"], writes=["pa%d" % p])
                for kc in range(KC):
                    sc.op("pe", lambda e, kc=kc, wb=wb, pb=pb, ts=ts: e.matmul(
                        pb[:], lhsT=wb[:, 1, kc, :], rhs=self.hT[:, kc, ts], start=(kc == 0), stop=(kc == KC - 1)),
                        reads=["w13b%d" % i, "hT"], writes=["pb%d" % p])
                si = self.n_sil % 2
                self.n_sil += 1
                sil = self.sil[si]
                sc.op("act", lambda e, sil=sil, pa=pa: e.activation(out=sil[:], in_=pa[:], func=AF.Silu),
                      reads=["pa%d" % p], writes=["sil%d" % si])
                sc.op("dve", lambda e, sil=sil, pb=pb, j=j, ts=ts: e.tensor_tensor(
                    out=self.u[:, j, ts], in0=sil[:], in1=pb[:], op=ALU.mult),
                    reads=["sil%d" % si, "pb%d" % p], writes=["u"])
        for m in range(KC):
            i = self.n_w2 % 2
            self.n_w2 += 1
            ws, wb = self.w2s[i], self.w2b[i]
            sc.op(q, lambda e, ws=ws, m=m: e.dma_start(out=ws[:], in_=w2_dram[m]), writes=["w2s%d" % i], dma=True)
            sc.op("pool", lambda e, ws=ws, wb=wb: e.tensor_copy(out=wb[:], in_=ws[:]),
                  reads=["w2s%d" % i], writes=["w2b%d" % i])
            for tt in range(ntile):
                ts = slice(tt * 512, (tt + 1) * 512)
                p = self.n_po % 2
                self.n_po += 1
                po = self.po[p]
                for j in range(NJ):
                    sc.op("pe", lambda e, j=j, wb=wb, po=po, ts=ts: e.matmul(
                        po[:], lhsT=wb[:, j, :], rhs=self.u[:, j, ts], start=(j == 0), stop=(j == NJ - 1)),
                        reads=["w2b%d" % i, "u"], writes=["po%d" % p])
                sc.op("dve", lambda e, po=po, m=m, ts=ts: e.scalar_tensor_tensor(
                    out=self.xT[:, m, ts], in0=po[:], scalar=0.5, in1=self.xT[:, m, ts],
                    op0=ALU.mult, op1=ALU.add),
                    reads=["po%d" % p, "xT"], writes=["xT"])


def _lay_w13(w1, w3):
    a = np.stack([w1, w3], axis=0).reshape(2, KC, 128, NJ, 128)
    return np.ascontiguousarray(a.transpose(3, 2, 0, 1, 4))


def _lay_w2(w2):
    a = w2.reshape(NJ, 128, KC, 128)
    return np.ascontiguousarray(a.transpose(2, 1, 0, 3))


def _lay_vec(v):
    return np.ascontiguousarray(v.reshape(KC, 128).T)


def build_phase1():
    nc = bass.Bass("TRN2", target_bir_lowering=False)
    xT_d = nc.dram_tensor("xT", [D, TOK], F32, kind="ExternalInput").ap()
    w13_d = nc.dram_tensor("w13", [NJ, 128, 2, KC, 128], F32, kind="ExternalInput").ap()
    w2_d = nc.dram_tensor("w2", [KC, 128, NJ, 128], F32, kind="ExternalInput").ap()
    gv_d = nc.dram_tensor("gv", [128, 4, KC], F32, kind="ExternalInput").ap()
    x1T_d = nc.dram_tensor("x1T", [D, TOK], F32, kind="ExternalOutput").ap()
    hT_d = nc.dram_tensor("hT", [D, TOK], BF16, kind="ExternalOutput").ap()
    with ExitStack() as es:
        sc = Sched(nc, es)
        c = FFNCtx(nc, es, sc)
        c.epsT = _mk(nc, es, "sb", "epsT", [128, 1], F32)
        sc.op("pool", lambda e: e.memset(c.epsT[:], RMS_EPS), writes=["eps"])
        sc.op("sp", lambda e: e.dma_start(out=c.gv[:], in_=gv_d), writes=["gv"], dma=True)
        xv = xT_d.rearrange("(kc p) t -> p kc t", p=128)
        x1v = x1T_d.rearrange("(kc p) t -> p kc t", p=128)
        hv = hT_d.rearrange("(kc p) t -> p kc t", p=128)
        for un in range(TOK // UNIT):
            us = slice(un * UNIT, (un + 1) * UNIT)
            sc.op("sp", lambda e, us=us: e.dma_start(out=c.xT[:], in_=xv[:, :, us]), writes=["xT"], dma=True)
            c.rmsnorm(0)
            c.ffn(w13_d, w2_d)
            c.rmsnorm(1)
            sc.op("sp", lambda e, us=us: e.dma_start(out=x1v[:, :, us], in_=c.xT[:]), reads=["xT"], dma=True)
            sc.op("sp", lambda e, us=us: e.dma_start(out=hv[:, :, us], in_=c.hT[:]), reads=["hT"], dma=True)
        sc.emit()
    return nc


def run_phase1(x, ffn1_norm, w1, w3, w2, mix_norm):
    nc = build_phase1()
    w13 = _lay_w13(w1, w3)
    w2l = _lay_w2(w2)
    gv = np.zeros((128, 4, KC), np.float32)
    gv[:, 0, :] = _lay_vec(ffn1_norm)
    gv[:, 1, :] = _lay_vec(mix_norm)
    xf = x.reshape(B * S, D)
    in_maps = []
    for c in range(NCORES):
        xT = np.ascontiguousarray(xf[c * TOK:(c + 1) * TOK].T)
        in_maps.append({"xT": xT, "w13": w13, "w2": w2l, "gv": gv})
    res = run_bass_kernel_spmd(nc, in_maps, core_ids=list(range(NCORES)))
    x1T = [r["x1T"] for r in res.results]
    hT = [r["hT"] for r in res.results]
    return x1T, hT


TT = 512
NT = S // TT
MCH = [("r0", 64), ("r1", 64), ("k0", 64), ("k1", 64), ("v0", 64), ("v1", 64),
       ("wlo", 32), ("alo", 32), ("glo", 96),
       ("fq0", 64), ("fq1", 64), ("fk0", 64), ("fk1", 64), ("fv", 128), ("ff", 2)]
MOFF = {}
_o = 0
for _n, _w in MCH:
    MOFF[_n] = (_o, _w)
    _o += _w
NCOL = _o
C0 = 0.6065306597126334
PV_W0, PV_A0, PV_KK, PV_KA, PV_RK, PV_LG, PV_LB, PV_QN, PV_KN, PV_OMKA, PV_QN8 = range(11)
NPV = 11


def build_phase2(nt=NT, stage=99):
    nc = bass.Bass("TRN2", target_bir_lowering=False)
    hT_d = nc.dram_tensor("hTb", [D, S], BF16, kind="ExternalInput").ap()
    wm_d = nc.dram_tensor("wmix", [128, KC, NCOL], F32, kind="ExternalInput").ap()
    mu_d = nc.dram_tensor("mu", [128, 9], F32, kind="ExternalInput").ap()
    pv_d = nc.dram_tensor("pv", [64, 2, NPV], F32, kind="ExternalInput").ap()
    wup_d = nc.dram_tensor("wup", [32, 2, 64], F32, kind="ExternalInput").ap()
    aup_d = nc.dram_tensor("aup", [32, 2, 64], F32, kind="ExternalInput").ap()
    gup_d = nc.dram_tensor("gup", [96, 2, 64], F32, kind="ExternalInput").ap()
    fb_d = nc.dram_tensor("fb", [2, 1], F32, kind="ExternalInput").ap()
    ya_d = nc.dram_tensor("yaT", [128, S], BF16, kind="ExternalOutput").ap()
    yb_d = nc.dram_tensor("ybT", [128, S], BF16, kind="ExternalOutput").ap()
    with ExitStack() as es:
        sc = Sched(nc, es, n_dma_sems=16)
        sb = lambda n, s_, d: _mk(nc, es, "sb", n, s_, d)
        ps = lambda n, s_, d: _mk(nc, es, "ps", n, s_, d)
        op = sc.op
        wmb = sb("wmb", [128, KC, NCOL], BF16)
        mu = sb("mu", [128, 9], F32)
        pv = sb("pv", [64, 2, NPV], F32)
        wup = sb("wup", [32, 2, 64], F32)
        aup = sb("aup", [32, 2, 64], F32)
        gup = sb("gup", [96, 2, 64], F32)
        fb = sb("fb", [2, 1], F32)
        nfb = sb("nfb", [2, 1], F32)
        for t_, d_, k_ in ((mu, mu_d, "mu"), (pv, pv_d, "pv"), (wup, wup_d, "wup"),
                           (aup, aup_d, "aup"), (gup, gup_d, "gup"), (fb, fb_d, "fb")):
            op("sp", lambda e, t_=t_, d_=d_: e.dma_start(out=t_[:], in_=d_), writes=[k_], dma=True)
        WMB = ["wmb%d" % kc for kc in range(KC)]
        for h in range(2):
            op("dve", lambda e, h=h: e.tensor_scalar(out=pv[:, h, PV_OMKA:PV_OMKA + 1], in0=pv[:, h, PV_KA:PV_KA + 1],
                                                     scalar1=-1.0, scalar2=1.0, op0=ALU.mult, op1=ALU.add),
               reads=["pv"], writes=["pv"])
            op("dve", lambda e, h=h: e.tensor_scalar(out=pv[:, h, PV_QN8:PV_QN8 + 1], in0=pv[:, h, PV_QN:PV_QN + 1],
                                                     scalar1=0.125, scalar2=None, op0=ALU.mult),
               reads=["pv"], writes=["pv"])
        op("dve", lambda e: e.tensor_scalar(out=nfb[:], in0=fb[:], scalar1=-1.0, scalar2=None, op0=ALU.mult),
           reads=["fb"], writes=["nfb"])
        ones64 = sb("ones64", [64, 64], F32)
        ident = sb("ident", [64, 64], F32)
        mska = sb("mska", [64, 128], F32)
        msl = sb("msl", [64, 64], F32)
        cmask = sb("cmask", [64, TT], F32)
        ones2 = sb("ones2", [2, TT], F32)
        sel65 = sb("sel65", [65, 64], F32)
        epsr = sb("epsr", [64, 1], F32)
        epsl = sb("epsl", [64, 1], F32)
        op("pool", lambda e: e.memset(ones64[:], 1.0), writes=["ones64"])
        op("pool", lambda e: e.memset(ones2[:], 1.0), writes=["ones2"])
        op("pool", lambda e: e.memset(epsr[:], RMS_EPS), writes=["epsr"])
        op("pool", lambda e: e.memset(epsl[:], LNX_EPS), writes=["epsl"])
        op("pool", lambda e: e.memset(ident[:], 1.0), writes=["ident"])
        op("pool", lambda e: e.affine_select(out=ident[:], in_=ident[:], pattern=[[-1, 64]], compare_op=ALU.is_equal,
                                             fill=0.0, base=0, channel_multiplier=1), reads=["ident"], writes=["ident"])
        op("pool", lambda e: e.memset(mska[:], 1.0), writes=["mska"])
        op("pool", lambda e: e.affine_select(out=mska[:, 0:64], in_=mska[:, 0:64], pattern=[[1, 64]], compare_op=ALU.is_ge,
                                             fill=0.0, base=-1, channel_multiplier=-1), reads=["mska"], writes=["mska"])
        op("pool", lambda e: e.affine_select(out=mska[:, 64:128], in_=mska[:, 64:128], pattern=[[1, 64]],
                                             compare_op=ALU.is_ge, fill=0.0, base=0, channel_multiplier=-1),
           reads=["mska"], writes=["mska"])
        op("pool", lambda e: e.memset(msl[:], 1.0), writes=["msl"])
        op("pool", lambda e: e.affine_select(out=msl[:], in_=msl[:], pattern=[[-1, 64]], compare_op=ALU.is_ge,
                                             fill=0.0, base=-1, channel_multiplier=1), reads=["msl"], writes=["msl"])
        op("pool", lambda e: e.memset(cmask[:], 1.0), writes=["cmask"])
        op("pool", lambda e: e.memset(cmask[:].rearrange("p (c t) -> p c t", t=64)[:, :, 0:1], 0.0),
           reads=["cmask"], writes=["cmask"])
        op("pool", lambda e: e.memset(sel65[:], 0.0), writes=["sel65"])
        op("pool", lambda e: e.memset(sel65[64:65, :], 1.0), reads=["sel65"], writes=["sel65"])
        Kaug = [sb("Kaug%d" % h, [70, S], BF16) for h in range(2)]
        Vext = sb("Vext", [128, S // 128, 2, 65], BF16)
        Qaug = [[sb("Qaug%d_%d" % (h, i), [70, TT], BF16) for i in range(2)] for h in range(2)]
        op("pool", lambda e: e.memset(Vext[:], 1.0), writes=["Vext"])
        for h in range(2):
            op("pool", lambda e, h=h: e.memset(Kaug[h][64:70, :], 1.0), writes=["Kaug%d" % h])
            for i in range(2):
                op("pool", lambda e, h=h, i=i: e.memset(Qaug[h][i][64:70, :], 1.0), writes=["Qaug%d_%d" % (h, i)])
        hsb0_ = sb("hsb0", [128, KC, TT], BF16)
        hsb = [hsb0_, hsb0_]
        praw = sb("praw", [128, TT + 1], F32)
        halo = sb("halo", [128, 9], F32)
        op("pool", lambda e: e.memset(halo[:], 0.0), writes=["halo"])
        sh = {}
        for n_, w_ in MCH[:9]:
            sh[n_] = sb("sh_" + n_, [w_, TT], F32)
        shd = sb("shd", [128, TT], F32)
        for kc in range(KC):
            for c0_, c1_ in ((0, 512), (512, NCOL)):
                op("sp", lambda e, kc=kc, c0_=c0_, c1_=c1_: e.dma_start(out=shd[:, 0:c1_ - c0_], in_=wm_d[:, kc, c0_:c1_]),
                   writes=["shd"], dma=True)
                op("dve", lambda e, kc=kc, c0_=c0_, c1_=c1_: e.tensor_copy(out=wmb[:, kc, c0_:c1_], in_=shd[:, 0:c1_ - c0_]),
                   reads=["shd"], writes=["wmb%d" % kc])
        tmp = [sb("tmp%d" % i, [64, TT], F32) for i in range(8)]
        fx = [tmp[4 + i][0:2, :] for i in range(4)]
        c3 = sb("c3", [2, 3, TT], BF16)
        nc3 = sb("nc3", [2, 3, TT], BF16)
        carry = sb("carry", [2, 1], F32)
        op("pool", lambda e: e.memset(carry[:], 0.0), writes=["carry"])
        fsq = tmp[0]
        frs = tmp[1]
        pt = [sb("pt%d" % i, [128, TT], BF16) for i in range(2)]
        osb = sb("osb", [65, TT], F32)
        clampb = shd
        rden = tmp[2]
        ybt = [sb("ybt%d" % i, [64, TT], BF16) for i in range(2)]
        def per_head(n, shape, dt):
            t_ = sb("%s_0" % n, shape, dt)
            return [t_, t_]
        RA = per_head("RA", [64, 8, 2, 64], F32)
        Bt = per_head("Bt", [64, TT], F32)
        Kt = per_head("Kt", [64, TT], F32)
        Ein = per_head("Ein", [64, TT], F32)
        gsb = per_head("gsb", [64, TT], F32)
        bonus = per_head("bonus", [64, TT], F32)
        tmB = per_head("tmB", [64, 8, 64], F32)
        tmK = per_head("tmK", [64, 8, 64], F32)
        tmV = per_head("tmV", [64, 8, 64], F32)
        YT = per_head("YT", [64, TT], F32)
        ST = [[sb("ST%d_%d" % (h, i), [64, 64], F32) for i in range(2)] for h in range(2)]
        for h in range(2):
            op("pool", lambda e, h=h: e.memset(ST[h][0][:], 0.0), writes=["ST%d_0" % h])
        tw = sb("tw", [32, TT], F32)
        sg = sb("sg", [96, TT], F32)
        yat = [sb("yat%d" % i, [64, TT], BF16) for i in range(2)]
        def small(n, w):
            return [[sb("%s%d_%d" % (n, h, i), [64, w], F32) for i in range(2)] for h in range(2)]
        Msb = small("Msb", 128)
        Lk = small("Lk", 128)
        Mj = small("Mj", 64)
        MTj = small("MTj", 64)
        Pm = small("Pm", 64)
        Wsb = small("Wsb", 64)
        Usb = small("Usb", 64)
        pp = ps("pp", [128, TT], F32)
        pm = ps("pm", [128, TT], F32)
        pst = [ps("pst%d" % i, [128, TT], F32) for i in range(2)]
        poT = ps("poT", [128, TT], F32)
        rkA = [ps("rkA%d" % h, [64, TT], F32) for h in range(1)]
        rkB = ps("rkB", [64, TT], F32)
        pm2 = ps("pm2", [128, TT], F32)

        hv = hT_d.rearrange("(kc p) t -> p kc t", p=128)
        cnt = {"pt": 0, "st": 0}

        def _blk1(it):
            tsl = slice(it * TT, (it + 1) * TT)
            hb = hsb[it % 2]
            hk = "hsb0"
            op("sp", lambda e, hb=hb, tsl=tsl: e.dma_start(out=hb[:], in_=hv[:, :, tsl]), writes=[hk], dma=True)

            def proj(name, dst_ps):
                o, w = MOFF[name]
                for kc in range(KC):
                    op("pe", lambda e, kc=kc, o=o, w=w: e.matmul(dst_ps[0:w, :], lhsT=wmb[:, kc, o:o + w], rhs=hb[:, kc, :],
                                                                 start=(kc == 0), stop=(kc == KC - 1)),
                       reads=[WMB[kc], hk], writes=["pp"])
            for ci, (name, w) in enumerate(MCH[:9]):
                proj(name, pp)
                pk = "praw"
                op("act", lambda e, ci=ci, w=w: e.activation(out=praw[0:w, 1:TT + 1], in_=pp[0:w, :], func=AF.Copy),
                   reads=["pp"], writes=[pk])
                op("pool", lambda e, ci=ci, w=w: e.tensor_copy(out=praw[0:w, 0:1], in_=halo[0:w, ci:ci + 1]),
                   reads=["halo"], writes=[pk])
                op("dve", lambda e, ci=ci, w=w: e.tensor_tensor(out=shd[0:w, :], in0=praw[0:w, 0:TT],
                                                                in1=praw[0:w, 1:TT + 1], op=ALU.subtract),
                   reads=[pk], writes=["shd"])
                op("dve", lambda e, ci=ci, w=w, name=name: e.scalar_tensor_tensor(
                    out=sh[name][:], in0=shd[0:w, :], scalar=mu[0:w, ci:ci + 1], in1=praw[0:w, 1:TT + 1],
                    op0=ALU.mult, op1=ALU.add), reads=["shd", "mu", pk], writes=["sh_" + name])
                op("pool", lambda e, ci=ci, w=w: e.tensor_copy(out=halo[0:w, ci:ci + 1], in_=praw[0:w, TT:TT + 1]),
                   reads=[pk], writes=["halo"])
            if stage < 1:
                return
            def _blk2(h):
                qa = Qaug[h][it % 2]
                qk_ = "Qaug%d_%d" % (h, it % 2)
                for which, dst, dk, pcol in (("fq%d" % h, qa[0:64, :], qk_, PV_QN8),
                                             ("fk%d" % h, Kaug[h][0:64, tsl], "Kaug%d" % h, PV_KN)):
                    proj(which, pp)
                    op("act", lambda e: e.activation(out=fsq[:], in_=pp[0:64, :], func=AF.Square),
                       reads=["pp"], writes=["tmp0"])
                    op("pe", lambda e: e.matmul(pm[0:64, :], lhsT=ones64[:], rhs=fsq[:], start=True, stop=True),
                       reads=["ones64", "tmp0"], writes=["pm"])
                    op("act", lambda e: e.activation(out=frs[:], in_=pm[0:64, :], func=AF.Sqrt, scale=1.0 / 64,
                                                     bias=epsr[:, 0:1]), reads=["pm", "epsr"], writes=["tmp1"])
                    op("dve", lambda e: e.reciprocal(out=frs[:], in_=frs[:]), reads=["tmp1"], writes=["tmp1"])
                    op("dve", lambda e, dst=dst, pcol=pcol, h=h: e.scalar_tensor_tensor(
                        out=dst, in0=pp[0:64, :], scalar=pv[:, h, pcol:pcol + 1], in1=frs[:], op0=ALU.mult, op1=ALU.mult),
                        reads=["pp", "pv", "tmp1"], writes=[dk])
            for h in range(2):
                _blk2(h)
            if stage < 2:
                return
            proj("ff", pp)
            op("act", lambda e: e.activation(out=fx[0], in_=pp[0:2, :], func=AF.Exp, scale=-1.0, bias=nfb[:, 0:1]),
               reads=["pp", "nfb"], writes=["tmp4"])
            op("act", lambda e: e.activation(out=fx[0], in_=fx[0], func=AF.Ln, bias=1.0),
               reads=["tmp4"], writes=["tmp4"])
            op("dve", lambda e: e.tensor_tensor_scan(out=fx[1], data0=ones2[:], data1=fx[0], initial=carry[:, 0:1],
                                                     op0=ALU.mult, op1=ALU.add),
               reads=["ones2", "tmp4", "carry"], writes=["tmp5"])
            op("dve", lambda e: e.tensor_copy(out=carry[:], in_=fx[1][:, TT - 1:TT]), reads=["tmp5"], writes=["carry"])
            op("dve", lambda e: e.tensor_copy(out=c3[:, 0, :], in_=fx[1]), reads=["tmp5"], writes=["c3"])
            op("dve", lambda e: e.tensor_tensor(out=fx[2], in0=fx[1], in1=c3[:, 0, :], op=ALU.subtract),
               reads=["tmp5", "c3"], writes=["tmp6"])
            op("dve", lambda e: e.tensor_copy(out=c3[:, 1, :], in_=fx[2]), reads=["tmp6"], writes=["c3"])
            op("dve", lambda e: e.tensor_tensor(out=fx[3], in0=fx[2], in1=c3[:, 1, :], op=ALU.subtract),
               reads=["tmp6", "c3"], writes=["tmp7"])
            op("dve", lambda e: e.tensor_copy(out=c3[:, 2, :], in_=fx[3]), reads=["tmp7"], writes=["c3"])
            op("dve", lambda e: e.tensor_scalar(out=nc3[:], in0=c3[:], scalar1=-1.0, scalar2=None, op0=ALU.mult),
               reads=["c3"], writes=["nc3"])
            def _blk3(h):
                for s3 in range(3):
                    op("sp", lambda e, h=h, s3=s3: e.dma_start(out=Kaug[h][67 + s3:68 + s3, tsl], in_=c3[h:h + 1, s3, :]),
                       reads=["c3"], writes=["Kaug%d" % h], dma=True)
                    op("sp", lambda e, h=h, s3=s3: e.dma_start(out=Qaug[h][it % 2][64 + s3:65 + s3, :],
                                                               in_=nc3[h:h + 1, s3, :]),
                       reads=["nc3"], writes=["Qaug%d_%d" % (h, it % 2)], dma=True)
            for h in range(2):
                _blk3(h)
            if stage < 3:
                return
            o_fv, _ = MOFF["fv"]
            for sbk in range(4):
                for kc in range(KC):
                    op("pe", lambda e, kc=kc, sbk=sbk: e.matmul(pp[:, 0:128], lhsT=hb[:, kc, sbk * 128:(sbk + 1) * 128],
                                                                rhs=wmb[:, kc, o_fv:o_fv + 128],
                                                                start=(kc == 0), stop=(kc == KC - 1)),
                       reads=[WMB[kc], hk], writes=["pp"])
                op("act", lambda e, sbk=sbk: e.activation(out=Vext[:, it * 4 + sbk, :, 0:64],
                                                          in_=pp[:, 0:128].rearrange("p (h d) -> p h d", h=2), func=AF.Copy),
                   reads=["pp"], writes=["Vext"])
            if stage < 4:
                return
            def _blk4(h):
                qa = Qaug[h][it % 2]
                qk_ = "Qaug%d_%d" % (h, it % 2)
                nkb = 4 * (it + 1)
                for kb in range(nkb):
                    si = cnt["st"] % 2
                    cnt["st"] += 1
                    pi = cnt["pt"] % 2
                    cnt["pt"] += 1
                    op("pe", lambda e, kb=kb, si=si, h=h, qa=qa: e.matmul(
                        pst[si][:], lhsT=Kaug[h][0:70, kb * 128:(kb + 1) * 128], rhs=qa[0:70, :], start=True, stop=True),
                        reads=["Kaug%d" % h, qk_], writes=["pst%d" % si])
                    if kb >= 4 * it:
                        op("dve", lambda e, si=si: e.tensor_scalar(out=clampb[:], in0=pst[si][:], scalar1=40.0, scalar2=None,
                                                                   op0=ALU.min), reads=["pst%d" % si], writes=["shd"])
                        op("act", lambda e, pi=pi: e.activation(out=pt[pi][:], in_=clampb[:], func=AF.Exp),
                           reads=["shd"], writes=["pt%d" % pi])
                    else:
                        op("act", lambda e, si=si, pi=pi: e.activation(out=pt[pi][:], in_=pst[si][:], func=AF.Exp),
                           reads=["pst%d" % si], writes=["pt%d" % pi])
                    if kb >= 4 * it:
                        jb = kb - 4 * it
                        op("pool", lambda e, pi=pi, jb=jb: e.affine_select(
                            out=pt[pi][:], in_=pt[pi][:], pattern=[[1, TT]], compare_op=ALU.is_ge, fill=0.0,
                            base=-128 * jb, channel_multiplier=-1), reads=["pt%d" % pi], writes=["pt%d" % pi])
                    op("pe", lambda e, kb=kb, pi=pi, h=h, nkb=nkb: e.matmul(
                        poT[0:65, :], lhsT=Vext[:, kb, h, :], rhs=pt[pi][:], start=(kb == 0), stop=(kb == nkb - 1)),
                        reads=["Vext", "pt%d" % pi], writes=["poT"])
                op("act", lambda e: e.activation(out=osb[:], in_=poT[0:65, :], func=AF.Copy), reads=["poT"], writes=["osb"])
                op("pe", lambda e: e.matmul(pm[0:64, :], lhsT=sel65[:], rhs=osb[:], start=True, stop=True),
                   reads=["sel65", "osb"], writes=["pm"])
                op("dve", lambda e: e.reciprocal(out=rden[:], in_=pm[0:64, :]), reads=["pm"], writes=["tmp2"])
                yb_ = ybt[h]
                op("dve", lambda e, yb_=yb_: e.tensor_tensor(out=yb_[:], in0=osb[0:64, :], in1=rden[:], op=ALU.mult),
                   reads=["osb", "tmp2"], writes=["ybt%d" % h])
                op("sp", lambda e, yb_=yb_, h=h: e.dma_start(out=yb_d[h * 64:(h + 1) * 64, tsl], in_=yb_[:]),
                   reads=["ybt%d" % h], dma=True)
            for h in range(2):
                _blk4(h)
            if stage < 5:
                return
            op("act", lambda e: e.activation(out=tw[:], in_=sh["wlo"][:], func=AF.Tanh), reads=["sh_wlo"], writes=["tw"])
            op("act", lambda e: e.activation(out=sg[:], in_=sh["glo"][:], func=AF.Sigmoid), reads=["sh_glo"], writes=["sg"])
            def _blk5(h):
                H = "_0"
                r_, k_, v_ = sh["r%d" % h], sh["k%d" % h], sh["v%d" % h]
                rk_, kk_, vk_ = "sh_r%d" % h, "sh_k%d" % h, "sh_v%d" % h
                sig, cs, lr, kkn, kmod, t5, t6, t7 = tmp
                T = ["tmp%d" % i for i in range(8)]
                pcol = lambda c: pv[:, h, c:c + 1]
                v3 = lambda t_: t_[:].rearrange("p (c t) -> p c t", t=64)
                op("pe", lambda e, h=h: e.matmul(pm[0:64, :], lhsT=wup[:, h, :], rhs=tw[:], start=True, stop=True),
                   reads=["wup", "tw"], writes=["pm"])
                op("act", lambda e: e.activation(out=sig[:], in_=pm[0:64, :], func=AF.Sigmoid, bias=pcol(PV_W0)),
                   reads=["pm", "pv"], writes=[T[0]])
                op("dve", lambda e: e.tensor_tensor_scan(out=cs[:], data0=cmask[:], data1=sig[:], initial=0.0,
                                                         op0=ALU.mult, op1=ALU.add), reads=["cmask", T[0]], writes=[T[1]])
                op("act", lambda e, h=h: e.activation(out=Ein[h][:], in_=cs[:], func=AF.Exp, scale=-C0),
                   reads=[T[1]], writes=["Ein" + H])
                op("act", lambda e: e.activation(out=t5[:], in_=cs[:], func=AF.Exp, scale=C0), reads=[T[1]], writes=[T[5]])
                op("pool", lambda e: e.tensor_tensor(out=t6[:], in0=cs[:], in1=sig[:], op=ALU.subtract),
                   reads=[T[1], T[0]], writes=[T[6]])
                op("act", lambda e: e.activation(out=t6[:], in_=t6[:], func=AF.Exp, scale=-C0), reads=[T[6]], writes=[T[6]])
                if stage < 5.1:
                    return
                op("pe", lambda e, h=h: e.matmul(pm[0:64, :], lhsT=aup[:, h, :], rhs=sh["alo"][:], start=True, stop=True),
                   reads=["aup", "sh_alo"], writes=["pm"])
                op("act", lambda e: e.activation(out=lr[:], in_=pm[0:64, :], func=AF.Sigmoid, bias=pcol(PV_A0)),
                   reads=["pm", "pv"], writes=[T[2]])
                op("pe", lambda e, h=h: e.matmul(pm[0:64, :], lhsT=gup[:, h, :], rhs=sg[:], start=True, stop=True),
                   reads=["gup", "sg"], writes=["pm"])
                op("act", lambda e, h=h: e.activation(out=gsb[h][:], in_=pm[0:64, :], func=AF.Copy),
                   reads=["pm"], writes=["gsb" + H])
                if stage < 5.2:
                    return
                op("dve", lambda e: e.tensor_scalar(out=kkn[:], in0=k_[:], scalar1=pcol(PV_KK), scalar2=None, op0=ALU.mult),
                   reads=[kk_, "pv"], writes=[T[3]])
                op("act", lambda e: e.activation(out=t7[:], in_=kkn[:], func=AF.Square), reads=[T[3]], writes=[T[7]])
                op("pe", lambda e: e.matmul(pm[0:64, :], lhsT=ones64[:], rhs=t7[:], start=True, stop=True),
                   reads=["ones64", T[7]], writes=["pm"])
                op("act", lambda e: e.activation(out=t7[:], in_=pm[0:64, :], func=AF.Sqrt), reads=["pm"], writes=[T[7]])
                op("dve", lambda e: e.tensor_scalar(out=t7[:], in0=t7[:], scalar1=1e-12, scalar2=None, op0=ALU.max),
                   reads=[T[7]], writes=[T[7]])
                op("dve", lambda e: e.reciprocal(out=t7[:], in_=t7[:]), reads=[T[7]], writes=[T[7]])
                op("dve", lambda e: e.tensor_tensor(out=kkn[:], in0=kkn[:], in1=t7[:], op=ALU.mult),
                   reads=[T[3], T[7]], writes=[T[3]])
                if stage < 5.3:
                    return
                op("dve", lambda e: e.tensor_scalar(out=kmod[:], in0=lr[:], scalar1=pcol(PV_KA), scalar2=pcol(PV_OMKA),
                                                    op0=ALU.mult, op1=ALU.add), reads=[T[2], "pv"], writes=[T[4]])
                op("pool", lambda e: e.tensor_tensor(out=kmod[:], in0=kmod[:], in1=k_[:], op=ALU.mult),
                   reads=[T[4], kk_], writes=[T[4]])
                op("dve", lambda e, h=h: e.scalar_tensor_tensor(out=RA[h][:, :, 0, :], in0=v3(kkn), scalar=-1.0, in1=v3(t6),
                                                                op0=ALU.mult, op1=ALU.mult),
                   reads=[T[3], T[6]], writes=["RA" + H])
                op("pool", lambda e, h=h: e.tensor_tensor(out=RA[h][:, :, 1, :], in0=v3(r_), in1=v3(Ein[h]), op=ALU.mult),
                   reads=[rk_, "Ein" + H], writes=["RA" + H])
                op("pool", lambda e: e.tensor_tensor(out=t7[:], in0=kkn[:], in1=lr[:], op=ALU.mult),
                   reads=[T[3], T[2]], writes=[T[7]])
                op("dve", lambda e, h=h: e.tensor_tensor(out=Bt[h][:], in0=t7[:], in1=t5[:], op=ALU.mult),
                   reads=[T[7], T[5]], writes=["Bt" + H])
                op("pool", lambda e, h=h: e.tensor_tensor(out=Kt[h][:], in0=kmod[:], in1=t5[:], op=ALU.mult),
                   reads=[T[4], T[5]], writes=["Kt" + H])
                if stage < 5.4:
                    return
                op("dve", lambda e: e.scalar_tensor_tensor(out=t7[:], in0=r_[:], scalar=pcol(PV_RK), in1=kmod[:],
                                                           op0=ALU.mult, op1=ALU.mult), reads=[rk_, "pv", T[4]], writes=[T[7]])
                op("pe", lambda e: e.matmul(pm[0:64, :], lhsT=ones64[:], rhs=t7[:], start=True, stop=True),
                   reads=["ones64", T[7]], writes=["pm"])
                op("dve", lambda e, h=h: e.tensor_tensor(out=bonus[h][:], in0=pm[0:64, :], in1=v_[:], op=ALU.mult),
                   reads=["pm", vk_], writes=["bonus" + H])
                if stage < 5.5:
                    return
                for src, sk, dstT, dk in ((Bt[h], "Bt" + H, tmB[h], "tmB" + H), (Kt[h], "Kt" + H, tmK[h], "tmK" + H),
                                          (v_, vk_, tmV[h], "tmV" + H)):
                    for c in range(8):
                        op("pe", lambda e, c=c, src=src: e.transpose(out=pm2[0:64, c * 64:(c + 1) * 64],
                                                                      in_=src[:, c * 64:(c + 1) * 64], identity=ident[:]),
                           reads=[sk, "ident"], writes=["pm2"])
                    op("act", lambda e, dstT=dstT: e.activation(out=dstT[:].rearrange("p c t -> p (c t)"), in_=pm2[0:64, :],
                                                                func=AF.Copy), reads=["pm2"], writes=[dk])
                if stage < 6:
                    return
                A = rkA[0]
                def _blk6(c):
                    g_ = it * 8 + c
                    b2 = g_ % 2
                    csl = slice(c * 64, (c + 1) * 64)
                    st_old, st_new = ST[h][g_ % 2], ST[h][(g_ + 1) % 2]
                    sk_old, sk_new = "ST%d_%d" % (h, g_ % 2), "ST%d_%d" % (h, (g_ + 1) % 2)
                    ms, lk, wsb, usb = Msb[h][b2], Lk[h][b2], Wsb[h][b2], Usb[h][b2]
                    kms, klk, kws, kus = ("Msb%d_%d" % (h, b2), "Lk%d_%d" % (h, b2), "Wsb%d_%d" % (h, b2), "Usb%d_%d" % (h, b2))
                    ra_c = RA[h][:, c, :, :].rearrange("p a t -> p (a t)")
                    op("pe", lambda e, csl=csl, ra_c=ra_c, h=h: e.matmul(A[:, 0:128], lhsT=Bt[h][:, csl], rhs=ra_c,
                                                                         start=True, stop=True),
                       reads=["Bt" + H, "RA" + H], writes=["A"])
                    op("pe", lambda e, csl=csl, ra_c=ra_c, h=h: e.matmul(A[:, 128:256], lhsT=Kt[h][:, csl], rhs=ra_c,
                                                                         start=True, stop=True),
                       reads=["Kt" + H, "RA" + H], writes=["A"])
                    op("pe", lambda e, csl=csl, c=c, h=h: e.matmul(A[:, 256:320], lhsT=RA[h][:, c, 0, :], rhs=Bt[h][:, csl],
                                                                   start=True, stop=True),
                       reads=["Bt" + H, "RA" + H], writes=["A"])
                    op("dve", lambda e, ms=ms: e.tensor_tensor(out=ms[:], in0=A[:, 0:128], in1=mska[:], op=ALU.mult),
                       reads=["A", "mska"], writes=[kms])
                    op("dve", lambda e, lk=lk: e.tensor_tensor(out=lk[:], in0=A[:, 128:256], in1=mska[:], op=ALU.mult),
                       reads=["A", "mska"], writes=[klk])
                    mj, mtj, pmx = Mj[h][0], MTj[h][0], Pm[h][0]
                    kmj, kmtj, kpm = "Mj%d_0" % h, "MTj%d_0" % h, "Pm%d_0" % h
                    op("dve", lambda e, mtj=mtj: e.tensor_tensor(out=mtj[:], in0=A[:, 256:320], in1=msl[:], op=ALU.mult),
                       reads=["A", "msl"], writes=[kmtj])
                    op("pool", lambda e, ms=ms, pmx=pmx: e.tensor_tensor(out=pmx[:], in0=ms[:, 0:64], in1=ident[:], op=ALU.add),
                       reads=[kms, "ident"], writes=[kpm])
                    cur_m, cur_mk = ms[:, 0:64], kms
                    cur_mt, cur_mtk = mtj, kmtj
                    cur_p, cur_pk = pmx, kpm
                    bo = 0
                    if stage < 6.1:
                        return
                    for j in range(5):
                        nm, nmt, npm = Mj[h][(j + 1) % 2], MTj[h][(j + 1) % 2], Pm[h][(j + 1) % 2]
                        nmk, nmtk, npk = "Mj%d_%d" % (h, (j + 1) % 2), "MTj%d_%d" % (h, (j + 1) % 2), "Pm%d_%d" % (h, (j + 1) % 2)
                        if j < 4:
                            op("pe", lambda e, cur_m=cur_m, cur_mt=cur_mt, bo=bo: e.matmul(
                                rkB[:, bo:bo + 64], lhsT=cur_mt[:], rhs=cur_m, start=True, stop=True),
                                reads=[cur_mk, cur_mtk], writes=["B"])
                        op("pe", lambda e, cur_m=cur_m, cur_mt=cur_mt, bo=bo: e.matmul(
                            rkB[:, bo + 64:bo + 128], lhsT=cur_m, rhs=cur_mt[:], start=True, stop=True),
                            reads=[cur_mk, cur_mtk], writes=["B"])
                        if stage < 6.12:
                            continue
                        if j < 4 and stage != 6.122:
                            op("act", lambda e, nm=nm, bo=bo: e.activation(out=nm[:], in_=rkB[:, bo:bo + 64], func=AF.Copy),
                               reads=["B"], writes=[nmk])
                        if stage == 6.121:
                            continue
                        op("dve", lambda e, nmt=nmt, bo=bo: e.tensor_scalar(out=nmt[:], in0=rkB[:, bo + 64:bo + 128], scalar1=1.0, scalar2=None, op0=ALU.mult),
                           reads=["B"], writes=[nmtk])
                        if stage < 6.13:
                            continue
                        op("pe", lambda e, nmt=nmt, cur_p=cur_p, bo=bo: e.matmul(
                            rkB[:, bo + 128:bo + 192], lhsT=nmt[:], rhs=cur_p[:], start=True, stop=True),
                            reads=[nmtk, cur_pk], writes=["B"])
                        if stage < 6.14:
                            continue
                        op("dve", lambda e, npm=npm, cur_p=cur_p, bo=bo: e.tensor_tensor(
                            out=npm[:], in0=rkB[:, bo + 128:bo + 192], in1=cur_p[:], op=ALU.add),
                            reads=["B", cur_pk], writes=[npk])
                        cur_m, cur_mk = nm[:], nmk
                        cur_mt, cur_mtk = nmt, nmtk
                        cur_p, cur_pk = npm, npk
                    if stage < 6.2:
                        return
                    op("pe", lambda e, c=c, h=h, st_old=st_old: e.matmul(A[:, 320:384], lhsT=RA[h][:, c, 0, :], rhs=st_old[:],
                                                                         start=True, stop=True),
                       reads=["RA" + H, sk_old], writes=["A"])
                    op("pe", lambda e, c=c, h=h, lk=lk: e.matmul(A[:, 384:448], lhsT=lk[:, 0:64], rhs=tmV[h][:, c, :],
                                                                 start=True, stop=True),
                       reads=[klk, "tmV" + H], writes=["A"])
                    op("act", lambda e, wsb=wsb: e.activation(out=wsb[:], in_=A[:, 320:384], func=AF.Copy),
                       reads=["A"], writes=[kws])
                    op("dve", lambda e, wsb=wsb: e.tensor_tensor(out=wsb[:], in0=A[:, 384:448], in1=wsb[:], op=ALU.add),
                       reads=["A", kws], writes=[kws])
                    op("pe", lambda e, cur_p=cur_p, wsb=wsb: e.matmul(A[:, 448:512], lhsT=cur_p[:], rhs=wsb[:],
                                                                      start=True, stop=True),
                       reads=[cur_pk, kws], writes=["A"])
                    op("dve", lambda e, usb=usb: e.tensor_scalar(out=usb[:], in0=A[:, 448:512], scalar1=1.0, scalar2=None, op0=ALU.mult), reads=["A"], writes=[kus])
                    if stage < 6.3:
                        return
                    op("pe", lambda e, c=c, h=h, st_old=st_old: e.matmul(rkB[:, 192:256], lhsT=st_old[:], rhs=RA[h][:, c, 1, :],
                                                                         start=True, stop=True),
                       reads=["RA" + H, sk_old], writes=["B"])
                    op("pe", lambda e, usb=usb, ms=ms: e.matmul(rkB[:, 256:320], lhsT=usb[:], rhs=ms[:, 64:128],
                                                                start=True, stop=True), reads=[kus, kms], writes=["B"])
                    op("pe", lambda e, c=c, h=h, lk=lk: e.matmul(rkB[:, 320:384], lhsT=tmV[h][:, c, :], rhs=lk[:, 64:128],
                                                                 start=True, stop=True), reads=["tmV" + H, klk], writes=["B"])
                    op("pe", lambda e, c=c, h=h, usb=usb: e.matmul(rkB[:, 384:448], lhsT=tmB[h][:, c, :], rhs=usb[:],
                                                                   start=True, stop=True), reads=["tmB" + H, kus], writes=["B"])
                    op("pe", lambda e, c=c, h=h: e.matmul(rkB[:, 448:512], lhsT=tmK[h][:, c, :], rhs=tmV[h][:, c, :],
                                                          start=True, stop=True), reads=["tmK" + H, "tmV" + H], writes=["B"])
                    op("act", lambda e, csl=csl, h=h: e.activation(out=YT[h][:, csl], in_=rkB[:, 192:256], func=AF.Copy),
                       reads=["B"], writes=["YT" + H])
                    op("dve", lambda e, csl=csl, h=h: e.tensor_tensor(out=YT[h][:, csl], in0=rkB[:, 256:320], in1=YT[h][:, csl],
                                                                      op=ALU.add), reads=["B", "YT" + H], writes=["YT" + H])
                    op("dve", lambda e, csl=csl, h=h: e.tensor_tensor(out=YT[h][:, csl], in0=rkB[:, 320:384], in1=YT[h][:, csl],
                                                                      op=ALU.add), reads=["B", "YT" + H], writes=["YT" + H])
                    op("dve", lambda e, st_old=st_old, st_new=st_new: e.tensor_tensor(out=st_new[:], in0=rkB[:, 384:448],
                                                                                      in1=st_old[:], op=ALU.add),
                       reads=["B", sk_old], writes=[sk_new])
                    op("dve", lambda e, st_new=st_new: e.tensor_tensor(out=st_new[:], in0=rkB[:, 448:512], in1=st_new[:],
                                                                       op=ALU.add), reads=["B", sk_new], writes=[sk_new])
                    op("dve", lambda e, c=c, h=h, st_new=st_new: e.tensor_scalar(
                        out=st_new[:], in0=st_new[:], scalar1=Ein[h][:, c * 64 + 63:c * 64 + 64], scalar2=None,
                        op0=ALU.mult),
                        reads=[sk_new, "Ein" + H], writes=[sk_new])
                for c in range(8):
                    _blk6(c)
                if stage < 7:
                    return
                op("pe", lambda e, h=h: e.matmul(pm[0:64, :], lhsT=ones64[:], rhs=YT[h][:], start=True, stop=True),
                   reads=["ones64", "YT" + H], writes=["pm"])
                op("dve", lambda e, h=h: e.scalar_tensor_tensor(out=t5[:], in0=pm[0:64, :], scalar=-1.0 / 64, in1=YT[h][:],
                                                                op0=ALU.mult, op1=ALU.add), reads=["pm", "YT" + H], writes=[T[5]])
                op("act", lambda e: e.activation(out=t6[:], in_=t5[:], func=AF.Square), reads=[T[5]], writes=[T[6]])
                op("pe", lambda e: e.matmul(pm[0:64, :], lhsT=ones64[:], rhs=t6[:], start=True, stop=True),
                   reads=["ones64", T[6]], writes=["pm"])
                op("act", lambda e: e.activation(out=t6[:], in_=pm[0:64, :], func=AF.Sqrt, scale=1.0 / 64, bias=epsl[:, 0:1]),
                   reads=["pm", "epsl"], writes=[T[6]])
                op("dve", lambda e: e.reciprocal(out=t6[:], in_=t6[:]), reads=[T[6]], writes=[T[6]])
                op("dve", lambda e: e.scalar_tensor_tensor(out=t5[:], in0=t5[:], scalar=pcol(PV_LG), in1=t6[:],
                                                           op0=ALU.mult, op1=ALU.mult), reads=[T[5], "pv", T[6]], writes=[T[5]])
                op("dve", lambda e, h=h: e.scalar_tensor_tensor(out=t5[:], in0=t5[:], scalar=pcol(PV_LB), in1=bonus[h][:],
                                                                op0=ALU.add, op1=ALU.add),
                   reads=[T[5], "pv", "bonus" + H], writes=[T[5]])
                ya_ = yat[h]
                op("dve", lambda e, h=h, ya_=ya_: e.tensor_tensor(out=ya_[:], in0=t5[:], in1=gsb[h][:], op=ALU.mult),
                   reads=[T[5], "gsb" + H], writes=["yat%d" % h])
                op("sp", lambda e, ya_=ya_, h=h: e.dma_start(out=ya_d[h * 64:(h + 1) * 64, tsl], in_=ya_[:]),
                   reads=["yat%d" % h], dma=True)
            for h in range(2):
                _blk5(h)
        for it in range(nt):
            _blk1(it)
        sc.emit()
    return nc


def _phase2_inputs(inp, hT_list):
    w_in = inp["w_in"][0]
    maps = []
    hTb = [np.ascontiguousarray(np.concatenate(hT_list[b * 4:(b + 1) * 4], axis=1)) for b in range(B)]
    for c in range(NCORES):
        b, j = c // 4, c % 4
        h0, h1 = 2 * j, 2 * j + 1
        cols = []
        for base in (0, 512, 1024):
            cols += [np.arange(base + h0 * 64, base + h0 * 64 + 64), np.arange(base + h1 * 64, base + h1 * 64 + 64)]
        cols += [np.arange(1536, 1568), np.arange(1568, 1600), np.arange(1600, 1696)]
        fo = SHIFT_COLS
        for base in (fo, fo + 512):
            cols += [np.arange(base + h0 * 64, base + h0 * 64 + 64), np.arange(base + h1 * 64, base + h1 * 64 + 64)]
        cols += [np.arange(fo + 1024 + h0 * 64, fo + 1024 + h0 * 64 + 128)]
        cols += [np.arange(fo + 1536 + h0, fo + 1536 + h0 + 2)]
        cols = np.concatenate(cols)
        assert cols.shape[0] == NCOL
        wsel = w_in[:, cols]
        wmix = np.ascontiguousarray(wsel.reshape(KC, 128, NCOL).transpose(1, 0, 2))
        mu_all = inp["shift_mu"][0]
        mu = np.zeros((128, 9), np.float32)
        rcols = cols[:384 + 160]
        o = 0
        for ci, (n_, w_) in enumerate(MCH[:9]):
            mu[:w_, ci] = mu_all[rcols[o:o + w_]]
            o += w_
        pv = np.zeros((64, 2, NPV), np.float32)
        for hh, hg in enumerate((h0, h1)):
            sl = slice(hg * 64, hg * 64 + 64)
            pv[:, hh, PV_W0] = inp["rwkv_w0"][0][sl]
            pv[:, hh, PV_A0] = inp["rwkv_a0"][0][sl]
            pv[:, hh, PV_KK] = inp["rwkv_k_k"][0][sl]
            pv[:, hh, PV_KA] = inp["rwkv_k_a"][0][sl]
            pv[:, hh, PV_RK] = inp["rwkv_r_k"][0][hg]
            pv[:, hh, PV_LG] = inp["rwkv_lnx_g"][0][sl]
            pv[:, hh, PV_LB] = inp["rwkv_lnx_b"][0][sl]
            pv[:, hh, PV_QN] = inp["fox_q_norm"][0]
            pv[:, hh, PV_KN] = inp["fox_k_norm"][0]
        sl2 = slice(h0 * 64, h0 * 64 + 128)
        wup = np.ascontiguousarray(inp["rwkv_w_up"][0][:, sl2].reshape(32, 2, 64))
        aup = np.ascontiguousarray(inp["rwkv_a_up"][0][:, sl2].reshape(32, 2, 64))
        gup = np.ascontiguousarray(inp["rwkv_g_up"][0][:, sl2].reshape(96, 2, 64))
        fbv = np.ascontiguousarray(inp["fox_f_bias"][0][h0:h0 + 2].reshape(2, 1))
        maps.append({"hTb": hTb[b], "wmix": wmix, "mu": mu, "pv": pv, "wup": wup, "aup": aup, "gup": gup, "fb": fbv})
    return maps


def run_phase2(inp, hT_list):
    nc = build_phase2()
    maps = _phase2_inputs(inp, hT_list)
    res = run_bass_kernel_spmd(nc, maps, core_ids=list(range(NCORES)))
    return [r["yaT"] for r in res.results], [r["ybT"] for r in res.results]


def build_phase3():
    nc = bass.Bass("TRN2", target_bir_lowering=False)
    x1T_d = nc.dram_tensor("x1T", [D, TOK], F32, kind="ExternalInput").ap()
    hT_d = nc.dram_tensor("hT", [D, TOK], BF16, kind="ExternalInput").ap()
    yab_d = nc.dram_tensor("yab", [D, TOK], BF16, kind="ExternalInput").ap()
    wg_d = nc.dram_tensor("wg", [KC, 128, 2, KC, 128], F32, kind="ExternalInput").ap()
    wp_d = nc.dram_tensor("wp", [KC, 128, 2, 4, 128], F32, kind="ExternalInput").ap()
    wo_d = nc.dram_tensor("wo", [KC, 128, KC, 128], F32, kind="ExternalInput").ap()
    w13_d = nc.dram_tensor("w13", [NJ, 128, 2, KC, 128], F32, kind="ExternalInput").ap()
    w2_d = nc.dram_tensor("w2", [KC, 128, NJ, 128], F32, kind="ExternalInput").ap()
    gv_d = nc.dram_tensor("gv", [128, 4, KC], F32, kind="ExternalInput").ap()
    out_d = nc.dram_tensor("outT", [D, TOK], F32, kind="ExternalOutput").ap()
    with ExitStack() as es:
        sc = Sched(nc, es)
        c = FFNCtx(nc, es, sc)
        c.epsT = _mk(nc, es, "sb", "epsT", [128, 1], F32)
        op = sc.op
        op("pool", lambda e: e.memset(c.epsT[:], RMS_EPS), writes=["eps"])
        op("sp", lambda e: e.dma_start(out=c.gv[:], in_=gv_d), writes=["gv"], dma=True)
        xv = x1T_d.rearrange("(kc p) t -> p kc t", p=128)
        hv = hT_d.rearrange("(kc p) t -> p kc t", p=128)
        yv = yab_d.rearrange("(kc p) t -> p kc t", p=128)
        ov = out_d.rearrange("(kc p) t -> p kc t", p=128)
        mT = c.u[:, 0:KC, :]
        yab = c.u[:, KC:2 * KC, :]
        ntile = UNIT // 512

        def do_unit(un):
            us = slice(un * UNIT, (un + 1) * UNIT)
            op("sp", lambda e: e.dma_start(out=c.xT[:], in_=xv[:, :, us]), writes=["xT"], dma=True)
            op("sp", lambda e: e.dma_start(out=c.hT[:], in_=hv[:, :, us]), writes=["hT"], dma=True)
            op("sp", lambda e: e.dma_start(out=yab, in_=yv[:, :, us]), writes=["u"], dma=True)

            def do_m(m):
                i = c.n_w13 % 2
                c.n_w13 += 1
                ws, wb = c.w13s[i], c.w13b[i]
                i2 = c.n_w13 % 2
                c.n_w13 += 1
                ws2, wb2 = c.w13s[i2], c.w13b[i2]
                op("sp", lambda e: e.dma_start(out=ws[:], in_=wg_d[m]), writes=["w13s%d" % i], dma=True)
                op("pool", lambda e: e.tensor_copy(out=wb[:], in_=ws[:]), reads=["w13s%d" % i], writes=["w13b%d" % i])
                op("sp", lambda e: e.dma_start(out=ws2[:, :, 0:4, :], in_=wp_d[m]), writes=["w13s%d" % i2], dma=True)
                op("pool", lambda e: e.tensor_copy(out=wb2[:, :, 0:4, :], in_=ws2[:, :, 0:4, :]),
                   reads=["w13s%d" % i2], writes=["w13b%d" % i2])

                def do_tile(tt):
                    ts = slice(tt * 512, (tt + 1) * 512)
                    p = (m * ntile + tt) % 2
                    pa, pb, po0, po1 = c.pa[p], c.pb[p], c.po[0], c.po[1]
                    for kc in range(KC):
                        op("pe", lambda e, kc=kc: e.matmul(pa[:], lhsT=wb[:, 0, kc, :], rhs=c.hT[:, kc, ts],
                                                           start=(kc == 0), stop=(kc == KC - 1)),
                           reads=["w13b%d" % i, "hT"], writes=["pa%d" % p])
                    for kc in range(KC):
                        op("pe", lambda e, kc=kc: e.matmul(pb[:], lhsT=wb[:, 1, kc, :], rhs=c.hT[:, kc, ts],
                                                           start=(kc == 0), stop=(kc == KC - 1)),
                           reads=["w13b%d" % i, "hT"], writes=["pb%d" % p])
                    for kc in range(4):
                        op("pe", lambda e, kc=kc: e.matmul(po0[:], lhsT=wb2[:, 0, kc, :], rhs=yab[:, kc, ts],
                                                           start=(kc == 0), stop=(kc == 3)),
                           reads=["w13b%d" % i2, "u"], writes=["po0"])
                    for kc in range(4):
                        op("pe", lambda e, kc=kc: e.matmul(po1[:], lhsT=wb2[:, 1, kc, :], rhs=yab[:, 4 + kc, ts],
                                                           start=(kc == 0), stop=(kc == 3)),
                           reads=["w13b%d" % i2, "u"], writes=["po1"])
                    s0, s1 = c.sil[0], c.sil[1]
                    op("act", lambda e: e.activation(out=s0[:], in_=pa[:], func=AF.Sigmoid), reads=["pa%d" % p], writes=["sil0"])
                    op("act", lambda e: e.activation(out=s1[:], in_=pb[:], func=AF.Sigmoid), reads=["pb%d" % p], writes=["sil1"])
                    op("dve", lambda e: e.tensor_tensor(out=s0[:], in0=po0[:], in1=s0[:], op=ALU.mult),
                       reads=["po0", "sil0"], writes=["sil0"])
                    op("dve", lambda e: e.tensor_tensor(out=s1[:], in0=po1[:], in1=s1[:], op=ALU.mult),
                       reads=["po1", "sil1"], writes=["sil1"])
                    op("pool", lambda e: e.tensor_tensor(out=mT[:, m, ts], in0=s0[:], in1=s1[:], op=ALU.add),
                       reads=["sil0", "sil1"], writes=["mT"])
                for tt in range(ntile):
                    do_tile(tt)
            for m in range(KC):
                do_m(m)

            def do_wo(m):
                i = c.n_w13 % 2
                c.n_w13 += 1
                ws, wb = c.w13s[i], c.w13b[i]
                op("sp", lambda e: e.dma_start(out=ws[:, 0, :, :], in_=wo_d[m]), writes=["w13s%d" % i], dma=True)
                op("pool", lambda e: e.tensor_copy(out=wb[:, 0, :, :], in_=ws[:, 0, :, :]),
                   reads=["w13s%d" % i], writes=["w13b%d" % i])

                def do_tile(tt):
                    ts = slice(tt * 512, (tt + 1) * 512)
                    p = c.n_po % 2
                    c.n_po += 1
                    po = c.po[p]
                    for kc in range(KC):
                        op("pe", lambda e, kc=kc: e.matmul(po[:], lhsT=wb[:, 0, kc, :], rhs=mT[:, kc, ts],
                                                           start=(kc == 0), stop=(kc == KC - 1)),
                           reads=["w13b%d" % i, "mT"], writes=["po%d" % p])
                    op("dve", lambda e: e.tensor_tensor(out=c.xT[:, m, ts], in0=po[:], in1=c.xT[:, m, ts], op=ALU.add),
                       reads=["po%d" % p, "xT"], writes=["xT"])
                for tt in range(ntile):
                    do_tile(tt)
            for m in range(KC):
                do_wo(m)
            op("pool", lambda e: e.tensor_copy(out=c.sil[0][:, 0:1], in_=c.sil[0][:, 0:1]), reads=["mT", "sil0"], writes=["u", "sil0"])
            c.rmsnorm(2)
            c.ffn(w13_d, w2_d)
            c.rmsnorm(3, inplace=True)
            op("sp", lambda e: e.dma_start(out=ov[:, :, us], in_=c.xT[:]), reads=["xT"], dma=True)
        for un in range(TOK // UNIT):
            do_unit(un)
        sc.emit()
    return nc


def kernel(x, ffn1_norm, ffn1_w1, ffn1_w3, ffn1_w2, mix_norm, w_in, shift_mu,
           rwkv_w0, rwkv_w_up, rwkv_a0, rwkv_a_up, rwkv_g_up, rwkv_k_k, rwkv_k_a, rwkv_r_k,
           rwkv_lnx_g, rwkv_lnx_b, rwkv_proj, fox_f_bias, fox_q_norm, fox_k_norm, fox_proj,
           w_out, ffn2_norm, ffn2_w1, ffn2_w3, ffn2_w2, final_norm):
    inp = dict(x=x, w_in=w_in, shift_mu=shift_mu, rwkv_w0=rwkv_w0, rwkv_w_up=rwkv_w_up, rwkv_a0=rwkv_a0,
               rwkv_a_up=rwkv_a_up, rwkv_g_up=rwkv_g_up, rwkv_k_k=rwkv_k_k, rwkv_k_a=rwkv_k_a, rwkv_r_k=rwkv_r_k,
               rwkv_lnx_g=rwkv_lnx_g, rwkv_lnx_b=rwkv_lnx_b, fox_f_bias=fox_f_bias, fox_q_norm=fox_q_norm,
               fox_k_norm=fox_k_norm)
    inp = {k: np.asarray(v, dtype=np.float32) for k, v in inp.items()}
    x = inp["x"]
    f = lambda a: np.asarray(a, dtype=np.float32)
    x1T, hT = run_phase1(x, f(ffn1_norm)[0], f(ffn1_w1)[0], f(ffn1_w3)[0], f(ffn1_w2)[0], f(mix_norm)[0])
    ya, yb = run_phase2(inp, hT)
    nc = build_phase3()
    w_in0 = inp["w_in"][0]
    go = SHIFT_COLS + FOX_COLS
    wga, wgb = w_in0[:, go:go + D], w_in0[:, go + D:go + 2 * D]
    wg = np.ascontiguousarray(np.stack([wga, wgb], 0).reshape(2, KC, 128, KC, 128).transpose(3, 2, 0, 1, 4))
    pa, pb = f(rwkv_proj)[0], f(fox_proj)[0]
    wp = np.ascontiguousarray(np.stack([pa, pb], 0).reshape(2, 4, 128, KC, 128).transpose(3, 2, 0, 1, 4))
    wo = np.ascontiguousarray(f(w_out)[0].reshape(KC, 128, KC, 128).transpose(2, 1, 0, 3))
    w13 = _lay_w13(f(ffn2_w1)[0], f(ffn2_w3)[0])
    w2l = _lay_w2(f(ffn2_w2)[0])
    gv = np.zeros((128, 4, KC), np.float32)
    gv[:, 2, :] = _lay_vec(f(ffn2_norm)[0])
    gv[:, 3, :] = _lay_vec(f(final_norm))
    maps = []
    for c in range(NCORES):
        b, q = c // 4, c % 4
        qs = slice(q * TOK, (q + 1) * TOK)
        yab = np.concatenate([np.asarray(ya[4 * b + j])[:, qs] for j in range(4)] +
                             [np.asarray(yb[4 * b + j])[:, qs] for j in range(4)], axis=0)
        maps.append({"x1T": x1T[c], "hT": hT[c], "yab": np.ascontiguousarray(yab), "wg": wg, "wp": wp, "wo": wo,
                     "w13": w13, "w2": w2l, "gv": gv})
    res = run_bass_kernel_spmd(nc, maps, core_ids=list(range(NCORES)))
    out = np.concatenate([np.asarray(r["outT"]).T for r in res.results], axis=0)
    return np.ascontiguousarray(out.reshape(B, S, D).astype(np.float32))
```
